# Optimizing a Trainium2 kernel written in Bass

```python
import jax, jax.numpy as jnp
from jax import lax
import numpy as np

D_MODEL = 2048
BATCH = 1
SEQ = 8192
DEPTH = 4

GRID_W = 64
MLA_HEADS = 8
Q_LORA = 512
KV_LORA = 512
QK_NOPE = 128
QK_ROPE = 64
V_HEAD = 128
ROPE_THETA = 10000.0
Q_BLOCK = 128
NA_HEADS = 8
NA_HEAD_DIM = 128
NA_KH = 8
NA_KW = 16
D_FF = 4 * D_MODEL
EPS = 1e-6

MLA_W = MLA_HEADS * V_HEAD
NA_W = NA_HEADS * NA_HEAD_DIM
IN_WIDTHS = (Q_LORA, KV_LORA, QK_ROPE, NA_W, NA_W, NA_W, D_MODEL, D_MODEL)
IN_SPLITS = tuple(int(v) for v in np.cumsum(IN_WIDTHS)[:-1])
IN_TOTAL = int(sum(IN_WIDTHS))

kernel_name = "hybrid_mla_natten_sqrelu_encoder"


def rmsnorm(x, g):
    xf = x.astype(jnp.float32)
    y = xf * lax.rsqrt(jnp.mean(xf * xf, axis=-1, keepdims=True) + EPS)
    return (y * g.astype(jnp.float32)).astype(x.dtype)


def rope(x, cos, sin):
    x1, x2 = jnp.split(x, 2, axis=-1)
    return jnp.concatenate([x1 * cos - x2 * sin, x2 * cos + x1 * sin], axis=-1)


def mla_attention(q_nope, q_pe, k_nope, k_pe, v):
    B, S, H, _ = q_nope.shape
    nblk = S // Q_BLOCK
    scale = (QK_NOPE + QK_ROPE) ** -0.5
    qn_b = q_nope.reshape(B, nblk, Q_BLOCK, H, QK_NOPE).transpose(1, 0, 2, 3, 4)
    qp_b = q_pe.reshape(B, nblk, Q_BLOCK, H, QK_ROPE).transpose(1, 0, 2, 3, 4)

    def block(args):
        qn, qp = args
        s = (jnp.einsum('bqhd,bkhd->bhqk', qn, k_nope)
             + jnp.einsum('bqhd,bkd->bhqk', qp, k_pe)).astype(jnp.float32) * scale
        p = jax.nn.softmax(s, axis=-1).astype(v.dtype)
        return jnp.einsum('bhqk,bkhd->bqhd', p, v)

    out = lax.map(block, (qn_b, qp_b))
    return out.transpose(1, 0, 2, 3, 4).reshape(B, S, H * V_HEAD)


def neighbourhood_attention(q, k, v, rpb):
    B, S, H, d = q.shape
    rows = S // GRID_W
    kh = min(NA_KH, rows)
    r = jnp.arange(rows)
    row_start = jnp.clip(r - kh // 2, 0, rows - kh)
    row_idx = row_start[:, None] + jnp.arange(kh)[None, :]
    c = jnp.arange(GRID_W)
    col_start = jnp.clip(c - NA_KW // 2, 0, GRID_W - NA_KW)
    col_ok = (c[None, :] >= col_start[:, None]) & (c[None, :] < col_start[:, None] + NA_KW)

    qg = q.reshape(B, rows, GRID_W, H, d)
    kg = k.reshape(B, rows, GRID_W, H, d)[:, row_idx]
    vg = v.reshape(B, rows, GRID_W, H, d)[:, row_idx]
    s = jnp.einsum('brqhd,brikhd->brhqik', qg, kg).astype(jnp.float32) * (d ** -0.5)

    dy = row_idx - r[:, None] + (NA_KH - 1)
    dx = jnp.clip(c[None, :] - c[:, None], -(NA_KW - 1), NA_KW - 1) + (NA_KW - 1)
    bias = rpb.astype(jnp.float32)[:, dy][..., dx]
    bias = bias.transpose(1, 0, 3, 2, 4)
    s = jnp.where(col_ok[:, None, :], s + bias[None], -jnp.inf)
    p = jax.nn.softmax(s.reshape(B, rows, H, GRID_W, kh * GRID_W), axis=-1)
    p = p.reshape(B, rows, H, GRID_W, kh, GRID_W).astype(v.dtype)
    out = jnp.einsum('brhqik,brikhd->brqhd', p, vg)
    return out.reshape(B, S, H * d)


def setup_inputs(seed: int = 0) -> dict:
    key = jax.random.key(seed)
    ks = jax.random.split(key, 16)
    f32 = jnp.float32

    def w(k, shape, fan_in):
        return jax.random.normal(k, shape, f32) * fan_in ** -0.5

    def gain(k, shape):
        return 1.0 + 0.01 * jax.random.normal(k, shape, f32)

    return {
        "x": jax.random.normal(ks[0], (BATCH, SEQ, D_MODEL), f32),
        "norm_mix": gain(ks[1], (DEPTH, D_MODEL)),
        "w_in": w(ks[2], (DEPTH, D_MODEL, IN_TOTAL), D_MODEL),
        "norm_qa": gain(ks[3], (DEPTH, Q_LORA)),
        "w_uq": w(ks[4], (DEPTH, Q_LORA, MLA_HEADS * (QK_NOPE + QK_ROPE)), Q_LORA),
        "norm_kva": gain(ks[5], (DEPTH, KV_LORA)),
        "w_ukv": w(ks[6], (DEPTH, KV_LORA, MLA_HEADS * (QK_NOPE + V_HEAD)), KV_LORA),
        "rpb": 0.02 * jax.random.normal(ks[7], (DEPTH, NA_HEADS, 2 * NA_KH - 1, 2 * NA_KW - 1), f32),
        "w_o_mla": w(ks[8], (DEPTH, MLA_W, D_MODEL), MLA_W),
        "w_o_na": w(ks[9], (DEPTH, NA_W, D_MODEL), NA_W),
        "w_out": w(ks[10], (DEPTH, D_MODEL, D_MODEL), D_MODEL),
        "norm_mlp": gain(ks[11], (DEPTH, D_MODEL)),
        "w_ff1": w(ks[12], (DEPTH, D_MODEL, D_FF), D_MODEL),
        "w_ff2": w(ks[13], (DEPTH, D_FF, D_MODEL), D_FF),
        "norm_final": gain(ks[14], (D_MODEL,)),
    }


def reference(x, norm_mix, w_in, norm_qa, w_uq, norm_kva, w_ukv, rpb, w_o_mla, w_o_na,
              w_out, norm_mlp, w_ff1, w_ff2, norm_final):
    B, S, _ = x.shape
    pos = jnp.arange(S, dtype=jnp.float32)
    inv_freq = 1.0 / (ROPE_THETA ** (jnp.arange(0, QK_ROPE, 2, dtype=jnp.float32) / QK_ROPE))
    ang = pos[:, None] * inv_freq[None, :]
    cos = jnp.cos(ang).astype(x.dtype)
    sin = jnp.sin(ang).astype(x.dtype)

    for l in range(DEPTH):
        u = rmsnorm(x, norm_mix[l])
        proj = u @ w_in[l]
        c_q, c_kv, k_pe, q_na, k_na, v_na, gate_a, gate_b = jnp.split(proj, IN_SPLITS, axis=-1)

        q = (rmsnorm(c_q, norm_qa[l]) @ w_uq[l]).reshape(B, S, MLA_HEADS, QK_NOPE + QK_ROPE)
        kv = (rmsnorm(c_kv, norm_kva[l]) @ w_ukv[l]).reshape(B, S, MLA_HEADS, QK_NOPE + V_HEAD)
        q_nope, q_pe = q[..., :QK_NOPE], q[..., QK_NOPE:]
        k_nope, v = kv[..., :QK_NOPE], kv[..., QK_NOPE:]
        q_pe = rope(q_pe, cos[:, None, :], sin[:, None, :])
        k_pe = rope(k_pe, cos, sin)
        y_a = mla_attention(q_nope, q_pe, k_nope, k_pe, v) @ w_o_mla[l]

        hs = (B, S, NA_HEADS, NA_HEAD_DIM)
        y_b = neighbourhood_attention(q_na.reshape(hs), k_na.reshape(hs), v_na.reshape(hs), rpb[l]) @ w_o_na[l]

        merged = jax.nn.sigmoid(gate_a) * y_a + jax.nn.sigmoid(gate_b) * y_b
        x = x + merged @ w_out[l]

        h = rmsnorm(x, norm_mlp[l]) @ w_ff1[l]
        x = x + jnp.square(jax.nn.relu(h)) @ w_ff2[l]

    return rmsnorm(x, norm_final)
```

```python
import bisect
import numpy as np
import ml_dtypes
import concourse.bass as bass
import concourse.mybir as mybir
from concourse.bass_utils import run_bass_kernel_spmd

F32 = mybir.dt.float32
BF16 = mybir.dt.bfloat16
U8 = mybir.dt.uint8
AF = mybir.ActivationFunctionType
ALU = mybir.AluOpType

NCORES = 8
D = 2048
SEQ = 8192
T = SEQ // NCORES
DEPTH = 4
GRID_W = 64
NH = 8
EPS = 1e-6
KB = 1024
NIN = 8320
C_CQ, C_CKV, C_KPE, C_QNA, C_KNA, C_VNA, C_GA, C_GB = 0, 512, 1024, 1152, 2176, 3200, 4224, 6272
WOFF = {}
WGRP = {"A": ("w_in", "w_o_mla", "w_o_na", "w_out"), "B": ("w_ff1", "w_ff2")}
_SH = {"w_in": (256, NIN), "w_o_mla": (128, 2048), "w_o_na": (128, 2048), "w_out": (256, 2048),
       "w_ff1": (256, 8192), "w_ff2": (1024, 2048)}
WBLOB = {}
for _g, _names in WGRP.items():
    _o = 0
    for _n in _names:
        _r, _c = _SH[_n]
        WOFF[_n] = (_o, _r, _c, _g)
        _o += _r * _c
    assert _o % 2048 == 0
    WBLOB[_g] = _o
WROWS = {g: WBLOB[g] // 2048 for g in WBLOB}
KV_KT = 0
KV_KPE = KV_KT + NH * 128 * T
KV_V = KV_KPE + 64 * T
KV_KNAF = KV_V + NH * 128 * 8 * 129
KV_KNAL = KV_KNAF + NH * 128 * 256
KV_VNAF = KV_KNAL + NH * 128 * 256
KV_VNAL = KV_VNAF + 128 * 2 * NH * 129
KVBLOB = KV_VNAL + 128 * 2 * NH * 129
KVCOLS = 2048
KVROWS = (KVBLOB + KVCOLS - 1) // KVCOLS
KVPAD = KVROWS * KVCOLS

NA_CH = {0: (0, 6), 1: (0, 6), 2: (2, 5), 3: (3, 5), 4: (4, 5), 5: (5, 5), 6: (6, 6), 7: (6, 6)}
NA_MOFF = {0: 0, 1: 6, 2: 12, 3: 12, 4: 12, 5: 12, 6: 17, 7: 23}


class IMap:
    def __init__(self):
        self.starts = [0]
        self.segs = [[0, 1 << 60, None, {}]]

    def _split(self, x):
        i = bisect.bisect_right(self.starts, x) - 1
        s = self.segs[i]
        if s[0] == x:
            return i
        new = [x, s[1], s[2], dict(s[3])]
        s[1] = x
        self.starts.insert(i + 1, x)
        self.segs.insert(i + 1, new)
        return i + 1

    def access(self, s, e, op, eng_key, write):
        i0 = self._split(s)
        i1 = self._split(e)
        deps = set()
        for i in range(i0, i1):
            seg = self.segs[i]
            if seg[2] is not None:
                deps.add((seg[2], 'w'))
            if write:
                for r in seg[3].values():
                    deps.add((r, 'r'))
        if write:
            self.segs[i0:i1] = [[s, e, op, {}]]
            self.starts[i0:i1] = [s]
        else:
            for i in range(i0, i1):
                self.segs[i][3][eng_key] = op
        return deps


def ap_range(ap):
    t = ap.tensor
    esz = mybir.dt.size(ap.dtype)
    dims = [list(d) for d in ap.ap]
    off = ap.offset
    sp = str(ap.space)
    if 'DRAM' in sp.upper() or 'HBM' in sp.upper():
        ext = sum((c - 1) * abs(st) for st, c in dims) + 1
        return ('d:' + t.name, off * esz, (off + ext) * esz)
    pstride = dims[0][0]
    if pstride <= 0:
        pstride = 1 << 40
    within = off % pstride if pstride < (1 << 40) else off
    ext = sum((c - 1) * abs(st) for st, c in dims[1:]) + 1
    return ('s:' + t.name, within * esz, (within + ext) * esz)


class Sched:
    COMPUTE = ('pe', 'act', 'dve', 'pool')

    def __init__(self, plan_only=False):
        self.plan_only = plan_only
        self.ops = []
        self.maps = {}
        self.ndma = 0

    def add(self, eng, fn, reads=(), writes=(), dma_sem=None, cc=False):
        if self.plan_only:
            return -1
        oid = len(self.ops)
        is_dma = dma_sem is not None
        eng_key = ('dma', oid) if is_dma else eng
        deps = set()
        rw = []
        for ap in reads:
            rw.append((ap if isinstance(ap, tuple) else ap_range(ap), False))
        for ap in writes:
            rw.append((ap if isinstance(ap, tuple) else ap_range(ap), True))
        for (space, s, e), w in rw:
            m = self.maps.get(space)
            if m is None:
                m = self.maps[space] = IMap()
            for d, kind in m.access(s, e, oid, eng_key, w):
                if d == oid:
                    continue
                dop = self.ops[d]
                if (not is_dma) and dop['sem'] is None and dop['eng'] == eng:
                    if eng == 'pe' or kind != 'w' or w:
                        continue
                deps.add(d)
        self.ops.append(dict(eng=eng, fn=fn, deps=deps, sem=dma_sem, cc=cc, sig=is_dma or cc))
        return oid

    def emit(self, nc):
        ops = self.ops
        for o in ops:
            for d in o['deps']:
                ops[d]['sig'] = True
        sem_names = []
        for o in ops:
            k = o['sem'] if o['sem'] is not None else ('eng', o['eng'])
            o['semk'] = k
            if k not in sem_names:
                sem_names.append(k)
        sems = {k: nc.alloc_semaphore(name="s%d" % i) for i, k in enumerate(sem_names)}
        cnt = {k: 0 for k in sem_names}
        for o in ops:
            if o['sig']:
                inc = 16 if (o['sem'] is not None and not o['cc']) else 1
                cnt[o['semk']] += inc
                o['inc'] = inc
            o['val'] = cnt[o['semk']]
        per_eng = {}
        for i, o in enumerate(ops):
            per_eng.setdefault(o['eng'], []).append(i)
        final = dict(cnt)

        def run(engine_obj, eng):
            waited = {}
            for i in per_eng.get(eng, []):
                o = ops[i]
                need = {}
                for d in o['deps']:
                    dk = ops[d]['semk']
                    v = ops[d]['val']
                    if v > need.get(dk, 0):
                        need[dk] = v
                for dk, v in need.items():
                    if waited.get(dk, 0) < v:
                        engine_obj.wait_ge(sems[dk], v)
                        waited[dk] = v
                inst = o['fn'](engine_obj)
                if o['sig']:
                    inst.then_inc(sems[o['semk']], o['inc'])
            if eng == 'sp':
                for k, v in final.items():
                    if v > 0 and waited.get(k, 0) < v:
                        engine_obj.wait_ge(sems[k], v)

        with nc.Block() as block:
            @block.tensor
            def _(e):
                run(e, 'pe')

            @block.scalar
            def _(e):
                run(e, 'act')

            @block.vector
            def _(e):
                run(e, 'dve')

            @block.gpsimd
            def _(e):
                run(e, 'pool')

            @block.sync
            def _(e):
                run(e, 'sp')


class Builder:
    def __init__(self, depth=DEPTH, plan=None, debug=None):
        self.depth = depth
        self.plan = plan
        self.debug = bool(debug)
        self.S = Sched(plan_only=plan is None)
        self.nc = bass.Bass("TRN2", target_bir_lowering=False)
        self.wspecs = []
        self.wi = 0
        self.wemitted = 0
        self.psrr = 0
        self.evrr = 0
        self.dsem = 0

    def mm(self, out, lhsT, rhs, start, stop, extra_w=()):
        self.S.add('pe', lambda e: e.matmul(out, lhsT, rhs, start=start, stop=stop, skip_group_check=True),
                   reads=[lhsT, rhs], writes=[out] + list(extra_w))

    def tr(self, out, in_, ident):
        self.S.add('pe', lambda e: e.transpose(out, in_, ident), reads=[in_, ident], writes=[out])

    def act(self, out, in_, func, scale=1.0, bias=0.0, eng='act'):
        rd = [in_] + ([] if isinstance(bias, (int, float)) else [bias])
        self.S.add('act', lambda e: e.activation(out, in_, func, bias=bias, scale=scale),
                   reads=rd, writes=[out])

    def tt(self, out, in0, in1, op, eng='dve'):
        self.S.add(eng, lambda e: e.tensor_tensor(out, in0, in1, op), reads=[in0, in1], writes=[out])

    def ts(self, out, in0, s1, s2, op0, op1=None, eng='dve'):
        rd = [in0] + [s for s in (s1, s2) if not isinstance(s, (int, float, type(None)))]
        if op1 is None:
            self.S.add(eng, lambda e: e.tensor_scalar(out, in0, s1, None, op0), reads=rd, writes=[out])
        else:
            self.S.add(eng, lambda e: e.tensor_scalar(out, in0, s1, s2, op0, op1), reads=rd, writes=[out])

    def stt(self, out, in0, scalar, in1, op0, op1):
        rd = [in0, in1] + ([] if isinstance(scalar, (int, float)) else [scalar])
        self.S.add('dve', lambda e: e.scalar_tensor_tensor(out, in0, scalar, in1, op0, op1), reads=rd, writes=[out])

    def copy(self, out, in_, eng=None):
        if eng is None:
            self.evrr ^= 1
            eng = 'act' if self.evrr else 'dve'
        if eng == 'act':
            self.act(out, in_, AF.Copy)
        else:
            self.S.add(eng, lambda e: e.tensor_copy(out, in_), reads=[in_], writes=[out])

    def recip(self, out, in_, extra_r=()):
        self.S.add('dve', lambda e: e.reciprocal(out, in_), reads=[in_] + list(extra_r), writes=[out])

    def pe_fence(self, bank):
        f = self.PS[:, bank, 510:512]
        self.mm(f, self.ident[:, 0:128], self.ident[:, 0:2], start=True, stop=True)
        return f


    def memset(self, ap, val, eng='dve'):
        self.S.add(eng, lambda e: e.memset(ap, val), writes=[ap])

    def dma(self, out, in_, q='sp', sem=None):
        if sem is None:
            sem = ('dma', self.dsem % 24)
            self.dsem += 1
        self.S.add(q, lambda e: e.dma_start(out=out, in_=in_), reads=[in_], writes=[out], dma_sem=sem)

    def dump(self, name, ap, dtype=F32):
        if not self.debug:
            return
        shp = [int(x) for x in ap.shape]
        d = self.nc.dram_tensor("dbg_" + name, shp, dtype, kind="ExternalOutput").ap()
        self.dma(d, ap, sem=('dbg', name))

    def sb(self, off, shape, dtype, parts=128):
        n = int(np.prod(shape)) * mybir.dt.size(dtype)
        assert off + n <= self.SBN, (off, n, self.SBN)
        v = self.SB[0:parts, off:off + n].bitcast(dtype)
        if len(shape) == 2:
            v = v.rearrange("p (a b) -> p a b", a=shape[0])
        elif len(shape) == 3:
            v = v.rearrange("p (a b c) -> p a b c", a=shape[0], b=shape[1])
        elif len(shape) == 4:
            v = v.rearrange("p (a b c d) -> p a b c d", a=shape[0], b=shape[1], c=shape[2])
        return v

    def ps(self, bank, n=512, dtype=F32, parts=128):
        v = self.PS[0:parts, bank, :]
        if dtype != F32:
            v = v.bitcast(dtype)
        return v[:, 0:n]

    def wsrc(self, spec, dst):
        kind = spec[0]
        l = spec[1]
        if kind == 'g':
            _, _, name, row0, kc, col0, nw = spec
            off, rpr, ncols, grp = WOFF[name]
            g = self.wfull[grp].ap().rearrange("x y -> (x y)").rearrange("(r q) -> r q", r=NCORES)
            g = g[:, off:off + rpr * ncols]
            s = rpr // 128
            g = g.rearrange("r (s p n) -> p r s n", p=128, n=ncols)
            c0 = row0 // 128
            if s >= kc:
                r = c0 // s
                return [(dst, g[:, r, (c0 % s):(c0 % s) + kc, col0:col0 + nw], 'sp')]
            r0 = c0 // s
            nr = kc // s
            if s == 1:
                return [(dst, g[:, r0:r0 + nr, 0, col0:col0 + nw], 'sp')]
            d4 = dst.rearrange("p (r s) n -> p r s n", s=s)
            if nr <= s:
                return [(d4[:, i, :, :], g[:, r0 + i, :, col0:col0 + nw], 'sp') for i in range(nr)]
            return [(d4[:, :, si, :], g[:, r0:r0 + nr, si, col0:col0 + nw], 'sp') for si in range(s)]
        if kind == 'uq':
            return [(dst, self.w_uq[l].rearrange("(k p) n -> p k n", p=128), 'pool')]
        if kind == 'ukv':
            return [(dst, self.w_ukv[l].rearrange("(k p) n -> p k n", p=128), 'pool')]
        raise ValueError(kind)

    def wtile(self, spec):
        self.wspecs.append(spec)
        if spec[0] == 'g':
            kc, nw = spec[4], spec[6]
        else:
            kc, nw = 4, 2048
        idx = self.wi
        self.wi += 1
        R = 3
        if self.plan is not None:
            assert self.plan[idx] == spec, (idx, self.plan[idx], spec)
            while self.wemitted < min(idx + R, len(self.plan)):
                j = self.wemitted
                sp = self.plan[j]
                kcj, nwj = (sp[4], sp[6]) if sp[0] == 'g' else (4, 2048)
                dst = self.sb(self.O_WR + (j % R) * 16 * KB, [kcj, nwj], BF16)
                for i_, (d_, s_, q) in enumerate(self.wsrc(sp, dst)):
                    self.dma(d_, s_, q=q, sem=('w', j % R, i_))
                self.wemitted += 1
        return self.sb(self.O_WR + (idx % R) * 16 * KB, [kc, nw], BF16)

    def ss_stats(self, src_fn, nchunks, rstd, sq_off, inv_n):
        banks = (0, 1)
        k = 0
        for tb in range(2):
            for c in range(nchunks):
                sq = self.sb(sq_off + (k % 2) * 2 * KB, [512], F32)
                k += 1
                self.act(sq[:, 0, :] if len(sq.shape) == 3 else sq, src_fn(c, tb), AF.Square)
                self.mm(self.ps(banks[tb]), self.ones32, sq, start=(c == 0), stop=(c == nchunks - 1))
        for tb in range(2):
            self.act(rstd[:, tb * 512:(tb + 1) * 512], self.ps(banks[tb]), AF.Sqrt, scale=inv_n, bias=self.epsb)
        self.recip(rstd, rstd)

    def linear_T(self, wt, kc, in_fn, nchunks, epi, m=128, banks=(2, 3, 4, 5)):
        for j in range(nchunks):
            for tb in range(2):
                bank = banks[self.psrr % len(banks)]
                self.psrr += 1
                pst = self.ps(bank, parts=m)
                for k in range(kc):
                    self.mm(pst, wt[:, k, j * m:(j + 1) * m], in_fn(k, tb), start=(k == 0), stop=(k == kc - 1))
                epi(j, tb, pst)

    def build(self):
        nc = self.nc
        L = self.depth
        self.xT_in = nc.dram_tensor("xT", [D, T], F32, kind="ExternalInput").ap()
        self.wsh = {g: [nc.dram_tensor("wsh%s%d" % (g, l), [WROWS[g], 2048], F32, kind="ExternalInput") for l in range(L)]
                    for g in WGRP}
        self.w_uq = [nc.dram_tensor("wuq%d" % l, [512, 2048], F32, kind="ExternalInput").ap() for l in range(L)]
        self.w_ukv = [nc.dram_tensor("wukv%d" % l, [512, 2048], F32, kind="ExternalInput").ap() for l in range(L)]
        gvec = nc.dram_tensor("gvec", [128, 256], F32, kind="ExternalInput").ap()
        rope_in = nc.dram_tensor("rope", [64, 2, T], F32, kind="ExternalInput").ap()
        ident_in = nc.dram_tensor("ident", [128, 128], BF16, kind="ExternalInput").ap()
        mtab_in = nc.dram_tensor("mtab", [128, 29 * 128], BF16, kind="ExternalInput").ap()
        ebraw = [nc.dram_tensor("ebraw%d" % l, [128, NH * 7 * 128], F32, kind="ExternalInput").ap() for l in range(L)]
        yT = nc.dram_tensor("yT", [D, T], F32, kind="ExternalOutput").ap()
        self.wblob = {g: nc.dram_tensor("wblob" + g, [WROWS[g], 2048], BF16) for g in WGRP}
        self.wfull = {g: nc.dram_tensor("wfull" + g, [NCORES * WROWS[g], 2048], BF16) for g in WGRP}
        kvblob1 = nc.dram_tensor("kvblob", [KVROWS, KVCOLS], BF16)
        kvfull1 = nc.dram_tensor("kvfull", [NCORES * KVROWS, KVCOLS], BF16)
        kvblob = [kvblob1] * L
        kvfull = [kvfull1] * L
        xscr = nc.dram_tensor("xscr", [128, 16 * T], F32).ap()
        uscr = nc.dram_tensor("uscr", [128, 16 * T], BF16).ap()
        self.SBN = 206 * KB
        self.SB = nc.alloc_sbuf_tensor("SB", [128, self.SBN], U8)
        self.PS = nc.alloc_psum_tensor("PS", [128, 8, 512], F32)
        O_C = 0
        self.O_WR = 18 * KB
        O_UT = 66 * KB
        O_M = 98 * KB
        M = lambda kb: O_M + int(kb * KB)
        gv = self.sb(O_C, [256], F32)
        self.ident = self.sb(O_C + 1024, [128], BF16)
        self.ones32 = self.sb(O_C + 1024 + 256, [128], F32)
        self.epsb = self.sb(O_C + 1024 + 768, [1], F32)
        ropeC = self.sb(O_C + 2 * KB, [2, T], F32, parts=64)
        mtab = self.sb(O_C + 10 * KB, [29 * 128], BF16)
        xT = self.sb(O_M, [16, T], F32)
        uT = self.sb(O_UT, [16, T], BF16)

        self.dma(gv, gvec)
        self.dma(self.ident, ident_in)
        self.dma(ropeC, rope_in)
        self.dma(mtab, mtab_in)
        self.memset(self.ones32, 1.0)
        self.memset(self.epsb, EPS)
        self.dma(xT, self.xT_in.rearrange("(k p) t -> p k t", p=128))

        def weight_gather(l, g):
            self.dma(self.wblob[g].ap(), self.wsh[g][l].ap(), q='pool', sem=('wc', g))
            self.S.add('pool', lambda e, g=g: e.collective_compute(
                "AllGather", ALU.bypass, replica_groups=[list(range(NCORES))],
                ins=[self.wblob[g].ap().opt()], outs=[self.wfull[g].ap().opt()]),
                reads=[self.wblob[g].ap()], writes=[self.wfull[g].ap()], dma_sem=('cc', 'w%s%d' % (g, l)), cc=True)

        weight_gather(0, 'A')
        weight_gather(0, 'B')

        def rmsnorm_x(goff, dst_fn):
            rstd = self.sb(M(100), [T], F32)
            self.ss_stats(lambda c, tb: xT[:, c, tb * 512:(tb + 1) * 512], 16, rstd, M(104), 1.0 / D)
            for c in range(16):
                self.stt(dst_fn(c), xT[:, c, :], gv[:, goff + c:goff + c + 1], rstd, ALU.mult, ALU.mult)
            self.dump("rstd_%d" % goff, rstd)

        for l in range(L):
            kvb = kvblob[l].ap().rearrange("x y -> (x y)")
            kvf = kvfull[l].ap().rearrange("x y -> (x y)").rearrange("(r q) -> r q", r=NCORES)
            if l == 1:
                self.dump("xl1", xT, F32)
            rmsnorm_x(l * 16, lambda c: uT[:, c, :])
            if l == 1:
                self.dump("ul1", uT, BF16)
            self.dma(xscr, xT.rearrange("p a b -> p (a b)"), sem=('xs', 0))
            uin = lambda k, tb: uT[:, k, tb * 512:(tb + 1) * 512]

            knaT = self.sb(M(0), [NH, 1536], BF16)
            Kst = self.sb(M(0), [NH, T], BF16)
            vna = self.sb(M(24.5), [12, NH, 129], BF16)
            Vst = self.sb(M(24.5), [NH, 8, 129], BF16)
            cT = self.sb(M(49), [4, T], F32)
            cnT = self.sb(M(65), [4, T], BF16)
            qnaT = self.sb(M(73), [NH, T], BF16)
            rstd2 = self.sb(M(89), [T], F32)
            SQ2 = M(93)
            kt1 = self.sb(M(97), [512], F32, parts=64)
            kt2 = self.sb(M(99), [512], F32, parts=64)
            kpeL = self.sb(M(101), [T], BF16, parts=64)

            wt = self.wtile(('g', l, 'w_in', 0, 16, C_CKV, 512))
            self.linear_T(wt, 16, uin, 4, lambda j, tb, p: self.copy(cT[:, j, tb * 512:(tb + 1) * 512], p))
            self.ss_stats(lambda c, tb: cT[:, c, tb * 512:(tb + 1) * 512], 4, rstd2, SQ2, 1.0 / 512)
            for c in range(4):
                self.stt(cnT[:, c, :], cT[:, c, :], gv[:, 192 + l * 4 + c:192 + l * 4 + c + 1], rstd2, ALU.mult, ALU.mult)
            self.dump("rstdkv_%d" % l, rstd2)

            def rope_epi(dst_fn):
                st = {}

                def epi(j, tb, p):
                    tsl = slice(tb * 512, (tb + 1) * 512)
                    if j == 0:
                        self.tt(kt1, p, ropeC[:, 0, tsl], ALU.mult)
                        st[tb] = 1
                    else:
                        self.tt(kt2, p, ropeC[:, 1, tsl], ALU.mult)
                        self.tt(dst_fn(tb), kt1, kt2, ALU.add)
                return epi

            wt = self.wtile(('g', l, 'w_in', 0, 16, C_KPE, 128))
            for tb in range(2):
                for j in range(2):
                    bank = (2, 3, 4, 5)[self.psrr % 4]
                    self.psrr += 1
                    pst = self.ps(bank, parts=64)
                    for k in range(16):
                        self.mm(pst, wt[:, k, j * 64:(j + 1) * 64], uin(k, tb), start=(k == 0), stop=(k == 15))
                    tsl = slice(tb * 512, (tb + 1) * 512)
                    if j == 0:
                        self.tt(kt1, pst, ropeC[:, 0, tsl], ALU.mult)
                    else:
                        self.tt(kt2, pst, ropeC[:, 1, tsl], ALU.mult)
                        self.tt(kpeL[:, tsl], kt1, kt2, ALU.add)
            self.dma(kvb[KV_KPE:KV_KPE + 64 * T].rearrange("(p t) -> p t", p=64), kpeL, sem=('kvw', 0))

            wt = self.wtile(('ukv', l))
            cin = lambda k, tb: cnT[:, k, tb * 512:(tb + 1) * 512]
            self.linear_T(wt, 4, cin, 8, lambda j, tb, p: self.copy(Kst[:, j, tb * 512:(tb + 1) * 512], p))
            self.dma(kvb[KV_KT:KV_KT + NH * 128 * T].rearrange("(h p t) -> p h t", h=NH, p=128), Kst, sem=('kvw', 1))
            if l == 1:
                self.dump("Kst1", Kst, BF16)
                self.dump("cn1", cnT, BF16)
                self.dump("c1", cT, F32)
            self.memset(Vst[:, :, :, 128:129], 1.0)
            for tile_ in range(8):
                for half in range(2):
                    bank = (2, 3, 4, 5)[self.psrr % 4]
                    self.psrr += 1
                    pst = self.ps(bank)
                    for k in range(4):
                        self.mm(pst, cnT[:, k, tile_ * 128:(tile_ + 1) * 128],
                                wt[:, k, 1024 + half * 512:1024 + (half + 1) * 512], start=(k == 0), stop=(k == 3))
                    self.copy(Vst[:, half * 4:(half + 1) * 4, tile_, 0:128], pst.rearrange("p (h d) -> p h d", h=4))
            self.dma(kvb[KV_V:KV_V + NH * 128 * 8 * 129].rearrange("(h p x) -> p h x", h=NH, p=128),
                     Vst.rearrange("p h a b -> p h (a b)"), sem=('kvw', 2))

            for half in range(2):
                wt = self.wtile(('g', l, 'w_in', 0, 16, C_KNA + half * 512, 512))
                self.linear_T(wt, 16, uin, 4, lambda j, tb, p, half=half: self.copy(
                    knaT[:, half * 4 + j, 256 + tb * 512:256 + (tb + 1) * 512], p))
            self.memset(vna[:, :, :, 128:129], 1.0)
            for half in range(2):
                wt = self.wtile(('g', l, 'w_in', 0, 16, C_VNA + half * 512, 512))
                for tile_ in range(8):
                    bank = (2, 3, 4, 5)[self.psrr % 4]
                    self.psrr += 1
                    pst = self.ps(bank)
                    for k in range(16):
                        self.mm(pst, uT[:, k, tile_ * 128:(tile_ + 1) * 128], wt[:, k, :], start=(k == 0), stop=(k == 15))
                    self.copy(vna[:, 2 + tile_, half * 4:(half + 1) * 4, 0:128], pst.rearrange("p (h d) -> p h d", h=4))
            self.dma(kvb[KV_KNAF:KV_KNAF + NH * 128 * 256].rearrange("(h p t) -> p h t", h=NH, p=128),
                     knaT[:, :, 256:512], sem=('kvw', 3))
            self.dma(kvb[KV_KNAL:KV_KNAL + NH * 128 * 256].rearrange("(h p t) -> p h t", h=NH, p=128),
                     knaT[:, :, 1024:1280], sem=('kvw', 4))
            self.dma(kvb[KV_VNAF:KV_VNAF + 128 * 2 * NH * 129].rearrange("(p x) -> p x", p=128),
                     vna[:, 2:4, :, :].rearrange("p a b c -> p (a b c)"), sem=('kvw', 5))
            self.dma(kvb[KV_VNAL:KV_VNAL + 128 * 2 * NH * 129].rearrange("(p x) -> p x", p=128),
                     vna[:, 8:10, :, :].rearrange("p a b c -> p (a b c)"), sem=('kvw', 6))

            kvb_all = ('d:' + kvblob[l].name, 0, KVPAD * 2)
            self.S.add('pool', lambda e, l=l: e.collective_compute(
                "AllGather", ALU.bypass, replica_groups=[list(range(NCORES))],
                ins=[kvblob[l].ap().opt()], outs=[kvfull[l].ap().opt()]),
                reads=[kvb_all], writes=[kvfull[l].ap()], dma_sem=('cc', 'kv%d' % l), cc=True)
            if l > 0:
                weight_gather(l, 'B')
            kvf_tok = ('d:' + kvfull[l].name, 0, NCORES * KVPAD * 2)

            def halo(e, dst, sec, n, pat, kw, nxt):
                if getattr(self, '_pid', None) is None:
                    self._pid = e.partition_id()
                r = (self._pid + (1 if nxt else NCORES - 1)) % NCORES
                src = kvf[bass.ds(r, 1), sec:sec + n].rearrange(pat, **kw)
                return e.dma_start(out=dst, in_=src)
            self.S.add('pool', lambda e: halo(e, knaT[:, :, 0:256], KV_KNAL, NH * 128 * 256, "o (h p t) -> p (o h) t", dict(h=NH, p=128), False),
                       reads=[kvf_tok], writes=[knaT[:, :, 0:256]], dma_sem=('halo', 0))
            self.S.add('pool', lambda e: halo(e, knaT[:, :, 1280:1536], KV_KNAF, NH * 128 * 256, "o (h p t) -> p (o h) t", dict(h=NH, p=128), True),
                       reads=[kvf_tok], writes=[knaT[:, :, 1280:1536]], dma_sem=('halo', 1))
            self.S.add('pool', lambda e: halo(e, vna[:, 0:2, :, :].rearrange("p a b c -> p (a b c)"), KV_VNAL, 128 * 2 * NH * 129, "o (p x) -> p (o x)", dict(p=128), False),
                       reads=[kvf_tok], writes=[vna[:, 0:2, :, :]], dma_sem=('halo', 2))
            self.S.add('pool', lambda e: halo(e, vna[:, 10:12, :, :].rearrange("p a b c -> p (a b c)"), KV_VNAF, 128 * 2 * NH * 129, "o (p x) -> p (o x)", dict(p=128), True),
                       reads=[kvf_tok], writes=[vna[:, 10:12, :, :]], dma_sem=('halo', 3))

            for half in range(2):
                wt = self.wtile(('g', l, 'w_in', 0, 16, C_QNA + half * 512, 512))
                self.linear_T(wt, 16, uin, 4, lambda j, tb, p, half=half: self.copy(
                    qnaT[:, half * 4 + j, tb * 512:(tb + 1) * 512], p))
            wt = self.wtile(('g', l, 'w_in', 0, 16, C_CQ, 512))
            self.linear_T(wt, 16, uin, 4, lambda j, tb, p: self.copy(cT[:, j, tb * 512:(tb + 1) * 512], p))
            self.dma(uscr, uT.rearrange("p a b -> p (a b)"), sem=('us', 0))
            self.ss_stats(lambda c, tb: cT[:, c, tb * 512:(tb + 1) * 512], 4, rstd2, SQ2, 1.0 / 512)
            for c in range(4):
                self.stt(cnT[:, c, :], cT[:, c, :], gv[:, 176 + l * 4 + c:176 + l * 4 + c + 1], rstd2, ALU.mult, ALU.mult)
            self.dump("rstdq_%d" % l, rstd2)

            OnaT = self.sb(O_UT, [NH, T], BF16)
            OmT = self.sb(O_UT + 16 * KB, [NH, T], BF16)
            ebst = self.sb(M(49), [7 * 128 * 4], F32)
            EB = self.sb(M(89), [NH, 7 * 128], BF16)
            for hh in range(2):
                self.dma(ebst, ebraw[l][:, hh * 4 * 896:(hh + 1) * 4 * 896], sem=('eb', 0))
                self.act(EB[:, hh * 4:(hh + 1) * 4, :].rearrange("p a b -> p (a b)"), ebst, AF.Exp)
            na_scale = 128.0 ** -0.5
            it = 0
            for pr in range(8):
                lc0, nj = NA_CH[pr]
                j0 = lc0 - pr + 1
                for h in range(NH):
                    sb0 = 0 if it % 2 == 0 else 2
                    S_ps = self.PS[:, sb0:sb0 + 2, :].rearrange("p a b -> p (a b)")[:, 0:nj * 128]
                    for jl in range(nj):
                        lc = lc0 + jl
                        self.mm(S_ps[:, jl * 128:(jl + 1) * 128], knaT[:, h, lc * 128:(lc + 1) * 128],
                                qnaT[:, h, pr * 128:(pr + 1) * 128], start=True, stop=True)
                    E = self.sb(M(103) + (it % 2) * 1536, [768], BF16)[:, 0:nj * 128]
                    self.act(E, S_ps, AF.Exp, scale=na_scale)
                    self.tt(E, E, EB[:, h, j0 * 128:(j0 + nj) * 128], ALU.mult)
                    mo = NA_MOFF[pr]
                    self.tt(E, E, mtab[:, mo * 128:(mo + nj) * 128], ALU.mult, eng='pool')
                    O_ps = self.ps(4 + it % 2, n=129)
                    for jl in range(nj):
                        lc = lc0 + jl
                        self.mm(O_ps, E[:, jl * 128:(jl + 1) * 128], vna[:, lc, h, :], start=(jl == 0), stop=(jl == nj - 1))
                    rec = self.sb(M(106) + (it % 2) * 4, [1], F32)
                    On = self.sb(M(106) + 64 + (it % 2) * 256, [128], BF16)
                    fz = self.pe_fence(6 + it % 2)
                    self.recip(rec, O_ps[:, 128:129], extra_r=[fz])
                    self.tt(On, O_ps[:, 0:128], rec.broadcast_to([128, 128]), ALU.mult)
                    T_ps = self.ps(6 + it % 2, n=128, dtype=BF16)
                    self.tr(T_ps, On, self.ident)
                    self.copy(OnaT[:, h, pr * 128:(pr + 1) * 128], T_ps)
                    it += 1

            qnT = self.sb(M(0), [NH, T], BF16)
            qpT = self.sb(M(16), [NH, T], BF16, parts=64)
            kpeA = self.sb(M(32), [SEQ], BF16, parts=64)
            qt1 = self.sb(M(80), [512], F32, parts=64)
            qt2 = self.sb(M(82), [512], F32, parts=64)
            self.dma(kpeA.rearrange("p (r t) -> p r t", r=NCORES),
                     kvf[:, KV_KPE:KV_KPE + 64 * T].rearrange("r (p t) -> p r t", p=64), sem=('kpe', 0))
            wt = self.wtile(('uq', l))
            cqin = lambda k, tb: cnT[:, k, tb * 512:(tb + 1) * 512]
            for h in range(NH):
                for tb in range(2):
                    tsl = slice(tb * 512, (tb + 1) * 512)
                    bank = (0, 1, 2, 3)[self.psrr % 4]
                    self.psrr += 1
                    pst = self.ps(bank)
                    for k in range(4):
                        self.mm(pst, wt[:, k, h * 256:h * 256 + 128], cqin(k, tb), start=(k == 0), stop=(k == 3))
                    self.copy(qnT[:, h, tsl], pst)
                    for j in range(2):
                        bank = (0, 1, 2, 3)[self.psrr % 4]
                        self.psrr += 1
                        pst = self.ps(bank, parts=64)
                        for k in range(4):
                            self.mm(pst, wt[:, k, h * 256 + 128 + j * 64:h * 256 + 192 + j * 64], cqin(k, tb),
                                    start=(k == 0), stop=(k == 3))
                        if j == 0:
                            self.tt(qt1, pst, ropeC[:, 0, tsl], ALU.mult)
                        else:
                            self.tt(qt2, pst, ropeC[:, 1, tsl], ALU.mult)
                            self.tt(qpT[:, h, tsl], qt1, qt2, ALU.add)

            mla_scale = 192.0 ** -0.5
            kvi = 0
            pti = 0
            for h in range(NH):
                obank = (5, 6, 7)
                oacc = lambda qt: self.PS[:, obank[qt // 3], (qt % 3) * 129:(qt % 3) * 129 + 129]
                started = set()
                for r in range(NCORES):
                    slot = M(48) + (kvi % 3) * (4 * KB + 64)
                    kvi += 1
                    KT = self.sb(slot, [T], BF16)
                    Vr = self.sb(slot + 2 * KB, [8, 129], BF16)
                    self.dma(KT, kvf[r, KV_KT + h * 128 * T:KV_KT + (h + 1) * 128 * T].rearrange("(p t) -> p t", p=128),
                             sem=('kvk', (kvi - 1) % 3))
                    self.dma(Vr.rearrange("p a b -> p (a b)"),
                             kvf[r, KV_V + h * 128 * 1032:KV_V + (h + 1) * 128 * 1032].rearrange("(p x) -> p x", p=128),
                             sem=('kvv', (kvi - 1) % 3))
                    for kc in range(8):
                        sb0 = (pti % 2) * 2
                        for tb in range(2):
                            tsl = slice(tb * 512, (tb + 1) * 512)
                            self.mm(self.ps(sb0 + tb), KT[:, kc * 128:(kc + 1) * 128], qnT[:, h, tsl], start=True, stop=False)
                            self.mm(self.ps(sb0 + tb, parts=128), kpeA[:, r * T + kc * 128:r * T + (kc + 1) * 128], qpT[:, h, tsl],
                                    start=False, stop=True)
                        PT = self.sb(M(73) + (pti % 3) * 2 * KB, [T], BF16)
                        pti += 1
                        self.act(PT, self.PS[:, sb0:sb0 + 2, :].rearrange("p a b -> p (a b)"), AF.Exp, scale=mla_scale)
                        for qt in range(8):
                            b = obank[qt // 3]
                            first = b not in started
                            started.add(b)
                            self.mm(oacc(qt), PT[:, qt * 128:(qt + 1) * 128], Vr[:, kc, :], start=first,
                                    stop=(r == NCORES - 1 and kc == 7))
                fz = self.pe_fence(4)
                for qt in range(8):
                    rec = self.sb(M(79) + qt * 4, [1], F32)
                    On = self.sb(M(79) + 64 + (qt % 2) * 256, [128], BF16)
                    o = oacc(qt)
                    self.recip(rec, o[:, 128:129], extra_r=[fz])
                    self.tt(On, o[:, 0:128], rec.broadcast_to([128, 128]), ALU.mult)
                    T_ps = self.ps(4, n=128, dtype=BF16)
                    self.tr(T_ps, On, self.ident)
                    self.copy(OmT[:, h, qt * 128:(qt + 1) * 128], T_ps)

            if l == 0:
                self.dump("OnaT", OnaT[:, :, 0:64], BF16)
                self.dump("OmT", OmT[:, :, 0:64], BF16)
                self.dump("qnT", qnT[:, :, 0:64], BF16)
                self.dump("cqn", cnT[:, :, 0:64], BF16)
            uT2 = self.sb(M(0), [16, T], BF16)
            sg = self.sb(M(32), [2, 4, T], BF16)
            t1 = self.sb(M(48), [4, T], F32)
            mT = self.sb(M(64), [16, T], BF16)
            self.dma(uT2.rearrange("p a b -> p (a b)"), uscr, sem=('us', 1))
            u2in = lambda k, tb: uT2[:, k, tb * 512:(tb + 1) * 512]
            for w in range(4):
                for g_, cbase in ((0, C_GA), (1, C_GB)):
                    wt = self.wtile(('g', l, 'w_in', 0, 16, cbase + w * 512, 512))
                    self.linear_T(wt, 16, u2in, 4, lambda j, tb, p, g_=g_: self.act(
                        sg[:, g_, j, tb * 512:(tb + 1) * 512], p, AF.Sigmoid))
                wt = self.wtile(('g', l, 'w_o_mla', 0, 8, w * 512, 512))
                self.linear_T(wt, 8, lambda k, tb: OmT[:, k, tb * 512:(tb + 1) * 512], 4,
                              lambda j, tb, p: self.tt(t1[:, j, tb * 512:(tb + 1) * 512], p, sg[:, 0, j, tb * 512:(tb + 1) * 512], ALU.mult))
                wt = self.wtile(('g', l, 'w_o_na', 0, 8, w * 512, 512))

                def epi_b(j, tb, p, w=w):
                    tsl = slice(tb * 512, (tb + 1) * 512)
                    tmp = self.sb(M(96) + (tb % 2) * 2 * KB, [512], F32)
                    self.tt(tmp, p, sg[:, 1, j, tsl], ALU.mult)
                    self.tt(mT[:, w * 4 + j, tsl], tmp, t1[:, j, tsl], ALU.add, eng='pool')
                self.linear_T(wt, 8, lambda k, tb: OnaT[:, k, tb * 512:(tb + 1) * 512], 4, epi_b)
            if l == 0:
                self.dump("mT", mT[:, :, 0:64], BF16)
                self.dump("uT2", uT2[:, :, 0:64], BF16)
            self.dma(xT.rearrange("p a b -> p (a b)"), xscr, sem=('xs', 1))
            for w in range(4):
                wt = self.wtile(('g', l, 'w_out', 0, 16, w * 512, 512))
                self.linear_T(wt, 16, lambda k, tb: mT[:, k, tb * 512:(tb + 1) * 512], 4,
                              lambda j, tb, p, w=w: self.tt(xT[:, w * 4 + j, tb * 512:(tb + 1) * 512], p,
                                                           xT[:, w * 4 + j, tb * 512:(tb + 1) * 512], ALU.add))

            if l + 1 < L:
                weight_gather(l + 1, 'A')
            if l == 0:
                self.dump("x1", xT[:, :, 0:64], F32)
            rmsnorm_x(64 + l * 16, lambda c: uT[:, c, :])
            hT = self.sb(M(64), [16, T], BF16)
            rk = 0
            for q in range(4):
                for w in range(4):
                    wt = self.wtile(('g', l, 'w_ff1', 0, 16, q * 2048 + w * 512, 512))

                    def epi_h(j, tb, p, w=w):
                        nonlocal rk
                        tsl = slice(tb * 512, (tb + 1) * 512)
                        r_ = self.sb(M(96) + (rk % 2) * 2 * KB, [512], F32)
                        rk += 1
                        self.act(r_, p, AF.Relu)
                        self.tt(hT[:, w * 4 + j, tsl], r_, r_, ALU.mult, eng=('dve' if rk % 2 else 'pool'))
                    self.linear_T(wt, 16, uin, 4, epi_h)
                for w in range(4):
                    wt = self.wtile(('g', l, 'w_ff2', q * 2048, 16, w * 512, 512))
                    self.linear_T(wt, 16, lambda k, tb: hT[:, k, tb * 512:(tb + 1) * 512], 4,
                                  lambda j, tb, p, w=w: self.tt(xT[:, w * 4 + j, tb * 512:(tb + 1) * 512], p,
                                                               xT[:, w * 4 + j, tb * 512:(tb + 1) * 512], ALU.add))

        rstd = self.sb(M(100), [T], F32)
        self.ss_stats(lambda c, tb: xT[:, c, tb * 512:(tb + 1) * 512], 16, rstd, M(104), 1.0 / D)
        yv = yT.rearrange("(k p) t -> p k t", p=128)
        for c in range(16):
            o = self.sb(M(64) + (c % 4) * 4 * KB, [T], F32)
            self.stt(o, xT[:, c, :], gv[:, 128 + c:129 + c], rstd, ALU.mult, ALU.mult)
            self.dma(yv[:, c, :], o, sem=('out', c % 4))
        if not self.S.plan_only:
            self.S.emit(nc)
        return nc


def _prep_inputs(x, norm_mix, w_in, norm_qa, w_uq, norm_kva, w_ukv, rpb, w_o_mla, w_o_na,
                 w_out, norm_mlp, w_ff1, w_ff2, norm_final, depth):
    f = np.float32
    x = np.asarray(x, f)
    w_in = np.asarray(w_in, f)
    cols = np.concatenate([np.arange(0, 1088), np.arange(1056, 1088), np.arange(1024, 1056), np.arange(1088, 8256)])
    shared = {}
    g = np.zeros((128, 256), f)
    nm, nl, nf = np.asarray(norm_mix, f), np.asarray(norm_mlp, f), np.asarray(norm_final, f)
    nq, nk = np.asarray(norm_qa, f), np.asarray(norm_kva, f)
    for l in range(depth):
        g[:, l * 16:(l + 1) * 16] = nm[l].reshape(16, 128).T
        g[:, 64 + l * 16:64 + (l + 1) * 16] = nl[l].reshape(16, 128).T
        g[:, 176 + l * 4:176 + (l + 1) * 4] = nq[l].reshape(4, 128).T
        g[:, 192 + l * 4:192 + (l + 1) * 4] = nk[l].reshape(4, 128).T
    g[:, 128:144] = nf.reshape(16, 128).T
    shared["gvec"] = g
    shared["ident"] = np.eye(128, dtype=f).astype(ml_dtypes.bfloat16)
    uq_cols = []
    for h in range(NH):
        b = h * 192
        uq_cols += list(range(b, b + 192)) + list(range(b + 160, b + 192)) + list(range(b + 128, b + 160))
    ukv_cols = [h * 256 + d for h in range(NH) for d in range(128)] + [h * 256 + 128 + d for h in range(NH) for d in range(128)]
    for l in range(depth):
        shared["wuq%d" % l] = np.ascontiguousarray(np.asarray(w_uq[l], f)[:, uq_cols])
        shared["wukv%d" % l] = np.ascontiguousarray(np.asarray(w_ukv[l], f)[:, ukv_cols])
    a_ = np.arange(2)[:, None, None, None, None]
    ck = np.arange(64)[None, :, None, None, None]
    jj = np.arange(7)[None, None, :, None, None]
    b_ = np.arange(2)[None, None, None, :, None]
    cq = np.arange(64)[None, None, None, None, :]
    dy = np.clip(2 * (jj - 1) + 3 + a_ - b_, 0, 14) + 0 * ck + 0 * cq
    dx = np.clip(ck - cq, -15, 15) + 15 + 0 * a_ + 0 * jj + 0 * b_
    for l in range(depth):
        r = np.asarray(rpb[l], f)
        t = r[:, dy, dx]
        t = t.transpose(1, 2, 0, 3, 4, 5).reshape(128, NH * 7 * 128)
        shared["ebraw%d" % l] = np.ascontiguousarray(t)
    per_core = []
    pos = np.arange(SEQ, dtype=np.float32)
    inv_freq = (1.0 / (10000.0 ** (np.arange(0, 64, 2, dtype=np.float32) / 64))).astype(f)
    ang = pos[:, None] * inv_freq[None, :]
    cos, sin = np.cos(ang).astype(f), np.sin(ang).astype(f)
    wsh_full = []
    for l in range(depth):
        mats = {"w_in": w_in[l][:, cols], "w_o_mla": np.asarray(w_o_mla[l], f), "w_o_na": np.asarray(w_o_na[l], f),
                "w_out": np.asarray(w_out[l], f), "w_ff1": np.asarray(w_ff1[l], f), "w_ff2": np.asarray(w_ff2[l], f)}
        wsh_full.append(mats)
    colok = np.zeros((64, 64), bool)
    for q in range(64):
        cs = min(max(q - 8, 0), 48)
        colok[q, cs:cs + 16] = True
    for c in range(NCORES):
        m = dict(shared)
        m["xT"] = np.ascontiguousarray(x[0, c * T:(c + 1) * T, :].T)
        for l in range(depth):
            for g_, names in WGRP.items():
                parts = []
                for name in names:
                    rpr = WOFF[name][1]
                    parts.append(wsh_full[l][name][c * rpr:(c + 1) * rpr].reshape(-1))
                m["wsh%s%d" % (g_, l)] = np.concatenate(parts).reshape(WROWS[g_], 2048)
        cs_, sn_ = cos[c * T:(c + 1) * T].T, sin[c * T:(c + 1) * T].T
        rope = np.zeros((64, 2, T), f)
        rope[:32, 0], rope[32:, 0] = cs_, cs_
        rope[:32, 1], rope[32:, 1] = -sn_, sn_
        m["rope"] = rope
        mt = np.zeros((128, 29, 2, 64), f)
        for pr in (0, 1, 2, 6, 7):
            lc0, nj = NA_CH[pr]
            for jl in range(nj):
                for a in range(2):
                    gk = 16 * c + 2 * (lc0 + jl - 2) + a
                    for b in range(2):
                        gq = 16 * c + 2 * pr + b
                        ws = min(max(gq - 4, 0), 120)
                        if ws <= gk < ws + 8 and 0 <= gk < 128:
                            mt[a * 64:(a + 1) * 64, NA_MOFF[pr] + jl, b, :] = colok.T
        m["mtab"] = mt.reshape(128, 29 * 128).astype(ml_dtypes.bfloat16)
        per_core.append(m)
    return per_core


_CACHE = {}


def _get_nc(depth, debug=None):
    key = (depth, bool(debug))
    if key not in _CACHE:
        b0 = Builder(depth=depth, plan=None, debug=debug)
        b0.build()
        b1 = Builder(depth=depth, plan=b0.wspecs, debug=debug)
        _CACHE[key] = b1.build()
    return _CACHE[key]


def kernel(x, norm_mix, w_in, norm_qa, w_uq, norm_kva, w_ukv, rpb, w_o_mla, w_o_na,
           w_out, norm_mlp, w_ff1, w_ff2, norm_final, _depth=DEPTH, _debug=False):
    in_maps = _prep_inputs(x, norm_mix, w_in, norm_qa, w_uq, norm_kva, w_ukv, rpb, w_o_mla, w_o_na,
                           w_out, norm_mlp, w_ff1, w_ff2, norm_final, _depth)
    nc = _get_nc(_depth, _debug)
    res = run_bass_kernel_spmd(nc, in_maps, core_ids=list(range(NCORES)))
    if _debug:
        return res.results
    out = np.empty((1, SEQ, D), np.float32)
    for c in range(NCORES):
        out[0, c * T:(c + 1) * T, :] = res.results[c]["yT"].T
    return out
```
